# Optimizing a Trainium2 kernel written in Bass

```python
import math
import jax, jax.numpy as jnp
from jax import lax
import numpy as np

D_MODEL = 1024
BATCH = 16
SEQ = 2048
DEPTH = 1

CHUNK = 64
Q_BLOCK = 128
ATTN_HEADS = 4
HEAD_DIM = 64
ATTN_WIDTH = ATTN_HEADS * 2 * HEAD_DIM
CONV_CH = 512
CONV_WIDTH = 31
FFN_DIM = 2816
FFN_CONV_WIDTH = 3
LN_EPS = 1e-5
DEEPNORM_ALPHA = (2.0 * DEPTH) ** 0.25
DEEPNORM_BETA = (8.0 * DEPTH) ** -0.25
IN_COLS = 3 * ATTN_WIDTH + 2 * CONV_CH + 2 * D_MODEL

kernel_name = "diffattn_conformer_gated_hybrid_deepnorm"


def layer_norm(x, g, b):
    x32 = x.astype(jnp.float32)
    mu = jnp.mean(x32, axis=-1, keepdims=True)
    var = jnp.mean(jnp.square(x32 - mu), axis=-1, keepdims=True)
    y = (x32 - mu) * lax.rsqrt(var + LN_EPS) * g.astype(jnp.float32) + b.astype(jnp.float32)
    return y.astype(x.dtype)


def causal_dwconv(x, w, b):
    k_width, ch = w.shape
    y = lax.conv_general_dilated(
        x, w.astype(x.dtype)[:, None, :], window_strides=(1,), padding=[(k_width - 1, 0)],
        dimension_numbers=('NWC', 'WIO', 'NWC'), feature_group_count=ch)
    return y + b.astype(x.dtype)


def diff_attention(q, k, v, lam, lambda_init, sub_gain):
    bsz, seq = q.shape[0], q.shape[1]
    scale = HEAD_DIM ** -0.5
    slopes = jnp.exp2(-8.0 * jnp.arange(1, ATTN_HEADS + 1, dtype=jnp.float32) / ATTN_HEADS)
    kpos = jnp.arange(seq)

    def block(i):
        start = i * Q_BLOCK
        qb = lax.dynamic_slice_in_dim(q, start, Q_BLOCK, axis=1)
        qpos = start + jnp.arange(Q_BLOCK)
        s = jnp.einsum('bqhmd,bkhmd->bhmqk', qb, k).astype(jnp.float32) * scale
        dist = jnp.abs(qpos[:, None] - kpos[None, :]).astype(jnp.float32)
        bias = -slopes[:, None, None] * dist
        allowed = (kpos[None, :] // CHUNK) <= (qpos[:, None] // CHUNK)
        s = jnp.where(allowed, s + bias[None, :, None], -jnp.inf)
        p = jax.nn.softmax(s, axis=-1)
        a = p[:, :, 0] - lam * p[:, :, 1]
        o = jnp.einsum('bhqk,bkhe->bqhe', a.astype(v.dtype), v)
        o32 = o.astype(jnp.float32)
        o32 = o32 * lax.rsqrt(jnp.mean(jnp.square(o32), axis=-1, keepdims=True) + LN_EPS)
        o32 = o32 * sub_gain.astype(jnp.float32) * (1.0 - lambda_init)
        return o32.astype(v.dtype)

    out = lax.map(block, jnp.arange(seq // Q_BLOCK))
    out = jnp.moveaxis(out, 0, 1)
    return out.reshape(bsz, seq, ATTN_WIDTH)


def setup_inputs(seed: int = 0) -> dict:
    key = jax.random.key(seed)
    ks = jax.random.split(key, 24)
    L, D, A, C, F = DEPTH, D_MODEL, ATTN_WIDTH, CONV_CH, FFN_DIM
    beta = DEEPNORM_BETA
    nrm = lambda k, shape: jax.random.normal(k, shape, dtype=jnp.float32)
    col_scale = jnp.concatenate([
        jnp.ones((2 * A,), jnp.float32),
        jnp.full((A,), beta, jnp.float32),
        jnp.ones((2 * C + 2 * D,), jnp.float32),
    ])
    return {
        "x": nrm(ks[0], (BATCH, SEQ, D)),
        "w_in": nrm(ks[1], (L, D, IN_COLS)) * (D ** -0.5) * col_scale,
        "b_gate": 0.02 * nrm(ks[2], (L, 2 * D)),
        "lambda_q1": 0.1 * nrm(ks[3], (L, HEAD_DIM)),
        "lambda_k1": 0.1 * nrm(ks[4], (L, HEAD_DIM)),
        "lambda_q2": 0.1 * nrm(ks[5], (L, HEAD_DIM)),
        "lambda_k2": 0.1 * nrm(ks[6], (L, HEAD_DIM)),
        "subln_gain": 1.0 + 0.02 * nrm(ks[7], (L, 2 * HEAD_DIM)),
        "w_attn_proj": nrm(ks[8], (L, A, D)) * (A ** -0.5) * beta,
        "conv_dw_w": nrm(ks[9], (L, CONV_WIDTH, C)) * (CONV_WIDTH ** -0.5),
        "conv_dw_b": 0.02 * nrm(ks[10], (L, C)),
        "conv_ln_g": 1.0 + 0.02 * nrm(ks[11], (L, C)),
        "conv_ln_b": 0.02 * nrm(ks[12], (L, C)),
        "w_conv_proj": nrm(ks[13], (L, C, D)) * (C ** -0.5) * beta,
        "b_conv_proj": 0.02 * nrm(ks[14], (L, D)),
        "w_out": nrm(ks[15], (L, D, D)) * (D ** -0.5) * beta,
        "ln1_g": 1.0 + 0.02 * nrm(ks[16], (L, D)),
        "ln1_b": 0.02 * nrm(ks[17], (L, D)),
        "w_ffn_in": nrm(ks[18], (L, D, 2 * F)) * (D ** -0.5) * beta,
        "ffn_dw_w": nrm(ks[19], (L, FFN_CONV_WIDTH, F)) * (FFN_CONV_WIDTH ** -0.5),
        "ffn_dw_b": 0.02 * nrm(ks[20], (L, F)),
        "w_ffn_down": nrm(ks[21], (L, F, D)) * (F ** -0.5) * beta,
        "ln2_g": 1.0 + 0.02 * nrm(ks[22], (L, D)),
        "ln2_b": 0.02 * nrm(ks[23], (L, D)),
    }


def reference(x, w_in, b_gate, lambda_q1, lambda_k1, lambda_q2, lambda_k2, subln_gain,
              w_attn_proj, conv_dw_w, conv_dw_b, conv_ln_g, conv_ln_b, w_conv_proj, b_conv_proj,
              w_out, ln1_g, ln1_b, w_ffn_in, ffn_dw_w, ffn_dw_b, w_ffn_down, ln2_g, ln2_b):
    bsz, seq = x.shape[0], x.shape[1]
    A, C, D, F = ATTN_WIDTH, CONV_CH, D_MODEL, FFN_DIM
    for l in range(DEPTH):
        lambda_init = 0.8 - 0.6 * math.exp(-0.3 * l)
        proj = jnp.einsum('bsd,dc->bsc', x, w_in[l])
        q, k, v, glu, g = jnp.split(proj, [A, 2 * A, 3 * A, 3 * A + 2 * C], axis=-1)
        q = q.reshape(bsz, seq, ATTN_HEADS, 2, HEAD_DIM)
        k = k.reshape(bsz, seq, ATTN_HEADS, 2, HEAD_DIM)
        v = v.reshape(bsz, seq, ATTN_HEADS, 2 * HEAD_DIM)
        gates = jax.nn.sigmoid((g + b_gate[l]).astype(jnp.float32)).astype(x.dtype)
        g_attn, g_conv = jnp.split(gates, 2, axis=-1)

        lam = (jnp.exp(jnp.sum(lambda_q1[l].astype(jnp.float32) * lambda_k1[l].astype(jnp.float32)))
               - jnp.exp(jnp.sum(lambda_q2[l].astype(jnp.float32) * lambda_k2[l].astype(jnp.float32)))
               + lambda_init)
        attn = diff_attention(q, k, v, lam, lambda_init, subln_gain[l])
        attn = jnp.einsum('bsa,ad->bsd', attn, w_attn_proj[l])

        u = glu[..., :C] * jax.nn.sigmoid(glu[..., C:])
        u = causal_dwconv(u, conv_dw_w[l], conv_dw_b[l])
        u = jax.nn.silu(layer_norm(u, conv_ln_g[l], conv_ln_b[l]))
        conv = jnp.einsum('bsc,cd->bsd', u, w_conv_proj[l]) + b_conv_proj[l]

        mixed = g_attn * attn + g_conv * conv
        y = jnp.einsum('bsd,de->bse', mixed, w_out[l])
        x = layer_norm(DEEPNORM_ALPHA * x + y, ln1_g[l], ln1_b[l])

        up = jnp.einsum('bsd,df->bsf', x, w_ffn_in[l])
        gate, val = jnp.split(up, [F], axis=-1)
        gate = causal_dwconv(gate, ffn_dw_w[l], ffn_dw_b[l])
        hid = jax.nn.gelu(gate, approximate=False) * val
        f = jnp.einsum('bsf,fd->bsd', hid, w_ffn_down[l])
        x = layer_norm(DEEPNORM_ALPHA * x + f, ln2_g[l], ln2_b[l])
    return x
```

```python
import math
from contextlib import ExitStack

import numpy as np
import concourse.bass as bass
import concourse.mybir as mybir
from concourse.bass_utils import run_bass_kernel_spmd

F32 = mybir.dt.float32
BF16 = mybir.dt.bfloat16
ALU = mybir.AluOpType
AF = mybir.ActivationFunctionType

NCORES = 8
D = 1024
SEQ = 2048
T = 512
NT = SEQ // T
NSEQ = 2
TOK = NSEQ * SEQ
FF = 2816
NF = FF // 128
HEADS = 4
CONVK = 31
HALO = CONVK - 1
LN_EPS = 1e-5
ALPHA = 2.0 ** 0.25
LAMBDA_INIT = 0.8 - 0.6 * math.exp(0.0)
QK_SCALE = 0.125
NEG = -30000.0
SLOPES = [2.0 ** (-8.0 * (h + 1) / HEADS) for h in range(HEADS)]

_P = {}
_off = 0
for _name, _n in [("b_gate", 16), ("conv_w", 4 * CONVK), ("conv_b", 4), ("conv_g", 4), ("conv_beta", 4),
                  ("b_conv_proj", 8), ("ln1_g", 8), ("ln1_b", 8), ("ln2_g", 8), ("ln2_b", 8),
                  ("ffn_w", NF * 3), ("ffn_b", NF), ("gain", 1),
                  ("lq1", 64), ("lk1", 64), ("lq2", 64), ("lk2", 64)]:
    _P[_name] = (_off, _n)
    _off += _n
NP_ = _off

_DP = {}
_off = 0
for _name, _n in [("b_gate_h", 16), ("conv_w_h", 4 * CONVK), ("bc_h", 8), ("gain08", 1), ("neg_lam", 1),
                  ("tmp64a", 64), ("tmp64b", 64), ("s1", 1), ("s2", 1), ("e1", 1), ("e2", 1)]:
    _DP[_name] = (_off, _n)
    _off += _n
NDP = _off

C_IDENT = 0
C_DH = 128
C_BIAS = 128 + 4 * 128
NCST = C_BIAS + 64

UNITS = []
UNITS += [("qk0", 4096), ("qk1", 4096), ("v", 4096), ("glu_a", 4096), ("glu_b", 4096)]
UNITS += [("mix%d" % dc, 3072) for dc in range(8)]
UNITS += [("wo0", 4096), ("wo1", 4096)]
for g in range(6):
    nch = 4 if g < 5 else 2
    UNITS += [("fg%d" % g, 8 * 128 * nch), ("fv%d" % g, 8 * 128 * nch)]
UNITS += [("dn%d" % dc, NF * 128) for dc in range(8)]
NUNITS = len(UNITS)
UOFF = []
_o = 0
for _n, _sz in UNITS:
    UOFF.append(_o)
    _o += _sz
WTOT = _o
SLOT = 4096
RING = 4


def _pack_weights(w_in, w_attn_proj, w_conv_proj, w_out, w_ffn_in, w_ffn_down):
    w_in = w_in[0]; wa = w_attn_proj[0]; wc = w_conv_proj[0]; wo = w_out[0]
    wf = w_ffn_in[0]; wd = w_ffn_down[0]
    out = np.empty((128, WTOT), np.float32)

    def kmaj(w, c0, c1):
        k = w.shape[0] // 128
        blk = w[:, c0:c1].reshape(k, 128, c1 - c0)
        return np.ascontiguousarray(blk.transpose(1, 0, 2)).reshape(128, k * (c1 - c0))

    blocks = {}
    blocks["qk0"] = kmaj(w_in, 0, 512)
    blocks["qk1"] = kmaj(w_in, 512, 1024)
    blocks["v"] = kmaj(w_in, 1024, 1536)
    blocks["glu_a"] = kmaj(w_in, 1536, 2048)
    blocks["glu_b"] = kmaj(w_in, 2048, 2560)
    for dc in range(8):
        ga = kmaj(w_in, 2560 + dc * 128, 2560 + (dc + 1) * 128)
        gc = kmaj(w_in, 3584 + dc * 128, 3584 + (dc + 1) * 128)
        a = kmaj(wa, dc * 128, (dc + 1) * 128)
        c = kmaj(wc, dc * 128, (dc + 1) * 128)
        blocks["mix%d" % dc] = np.concatenate([ga, gc, a, c], axis=1)
    blocks["wo0"] = kmaj(wo, 0, 512)
    blocks["wo1"] = kmaj(wo, 512, 1024)
    for g in range(6):
        nch = 4 if g < 5 else 2
        blocks["fg%d" % g] = kmaj(wf, g * 512, g * 512 + nch * 128)
        blocks["fv%d" % g] = kmaj(wf, FF + g * 512, FF + g * 512 + nch * 128)
    for dc in range(8):
        blocks["dn%d" % dc] = kmaj(wd, dc * 128, (dc + 1) * 128)
    for (name, sz), off in zip(UNITS, UOFF):
        b = blocks[name]
        assert b.shape == (128, sz), (name, b.shape, sz)
        out[:, off:off + sz] = b
    return out


def _pack_params(inp):
    par = np.zeros((128, NP_), np.float32)

    def put(name, arr):
        o, n = _P[name]
        assert arr.shape == (128, n), (name, arr.shape, n)
        par[:, o:o + n] = arr

    def cols(v):
        return np.ascontiguousarray(v.reshape(-1, 128).T)

    put("b_gate", cols(inp["b_gate"][0]))
    cw = inp["conv_dw_w"][0]
    put("conv_w", np.ascontiguousarray(cw.reshape(CONVK, 4, 128).transpose(2, 1, 0)).reshape(128, 4 * CONVK))
    put("conv_b", cols(inp["conv_dw_b"][0]))
    put("conv_g", cols(inp["conv_ln_g"][0]))
    put("conv_beta", cols(inp["conv_ln_b"][0]))
    put("b_conv_proj", cols(inp["b_conv_proj"][0]))
    put("ln1_g", cols(inp["ln1_g"][0])); put("ln1_b", cols(inp["ln1_b"][0]))
    put("ln2_g", cols(inp["ln2_g"][0])); put("ln2_b", cols(inp["ln2_b"][0]))
    fw = inp["ffn_dw_w"][0]
    put("ffn_w", np.ascontiguousarray(fw.reshape(3, NF, 128).transpose(2, 1, 0)).reshape(128, NF * 3))
    put("ffn_b", cols(inp["ffn_dw_b"][0]))
    put("gain", inp["subln_gain"][0].reshape(128, 1))
    for nm, key in [("lq1", "lambda_q1"), ("lk1", "lambda_k1"), ("lq2", "lambda_q2"), ("lk2", "lambda_k2")]:
        put(nm, np.broadcast_to(inp[key][0][None, :], (128, 64)))
    return par


def _make_consts():
    cst = np.zeros((128, NCST), np.float32)
    cst[:, C_IDENT:C_IDENT + 128] = np.eye(128, dtype=np.float32)
    kr = np.arange(128)[:, None].astype(np.float64)
    qr = np.arange(128)[None, :].astype(np.float64)
    for h in range(HEADS):
        sl = SLOPES[h]
        corr = 2.0 * sl * np.minimum(qr - kr, 0.0)
        mask = np.where((kr >= 64) & (qr < 64), NEG, 0.0)
        cst[:, C_DH + h * 128:C_DH + (h + 1) * 128] = (corr + mask) / QK_SCALE
        for mi in range(16):
            m = mi - 3
            cst[:, C_BIAS + h * 16 + mi] = sl * (np.arange(128) - 256.0 - 128.0 * m)
    return cst


class _Op:
    __slots__ = ("eng", "fn", "deps", "signal", "value", "group", "idx")

    def __init__(self, eng, fn, group, idx):
        self.eng = eng; self.fn = fn; self.deps = []; self.signal = False
        self.value = None; self.group = group; self.idx = idx


class Sched:
    ENGS = ("pe", "act", "dve", "pool", "sp")

    def __init__(self):
        self.ops = []
        self.last_w = {}
        self.readers = {}

    def op(self, eng, fn, reads=(), writes=(), group=None):
        idx = len(self.ops)
        o = _Op(eng, fn, group, idx)
        deps = set()
        for r in reads:
            if r in self.last_w:
                deps.add(self.last_w[r])
        for w in writes:
            if w in self.last_w:
                deps.add(self.last_w[w])
            for ri in self.readers.get(w, {}).values():
                deps.add(ri)
        for d in sorted(deps):
            od = self.ops[d]
            if od.group is None and od.eng == eng and eng == "pe" and group is None:
                continue
            o.deps.append(d)
            od.signal = True
        rkey = eng if group is None else ("dma", group)
        for r in reads:
            self.readers.setdefault(r, {})[rkey] = idx
        for w in writes:
            self.last_w[w] = idx
            self.readers[w] = {}
        self.ops.append(o)
        return o

    def emit(self, nc, stack):
        cnt = {}
        for o in self.ops:
            if o.group is not None:
                k = ("g", o.group)
                cnt[k] = cnt.get(k, 0) + 16
                o.value = (k, cnt[k])
            elif o.signal:
                k = ("e", o.eng)
                cnt[k] = cnt.get(k, 0) + 1
                o.value = (k, cnt[k])
        sems = {}
        for k in cnt:
            nm = "s_" + "_".join(str(x) for x in k)
            sems[k] = stack.enter_context(nc.semaphore(nm))
        per_eng = {e: [o for o in self.ops if o.eng == e] for e in self.ENGS}
        ops = self.ops

        def run(engobj, lst):
            known = {}
            for o in lst:
                need = {}
                for d in o.deps:
                    k, v = ops[d].value
                    if need.get(k, 0) < v:
                        need[k] = v
                for k, v in need.items():
                    if known.get(k, 0) < v:
                        engobj.wait_ge(sems[k], v)
                        known[k] = v
                if o.fn is None:
                    continue
                ins = o.fn(engobj)
                if o.value is not None:
                    k, v = o.value
                    ins.then_inc(sems[k], 16 if o.group is not None else 1)

        block = stack.enter_context(nc.Block())

        @block.sync
        def _(e):
            run(e, per_eng["sp"])

        @block.gpsimd
        def _(e):
            run(e, per_eng["pool"])

        @block.scalar
        def _(e):
            run(e, per_eng["act"])

        @block.vector
        def _(e):
            run(e, per_eng["dve"])

        @block.tensor
        def _(e):
            run(e, per_eng["pe"])


def build_nc():
    nc = bass.Bass("TRN2", target_bir_lowering=False)
    xT_d = nc.dram_tensor("xT", [NSEQ * NT, 128, 8 * T], F32, kind="ExternalInput").ap()
    wst_d = nc.dram_tensor("wst", [128, WTOT], F32, kind="ExternalInput").ap()
    par_d = nc.dram_tensor("par", [128, NP_], F32, kind="ExternalInput").ap()
    cst_d = nc.dram_tensor("cst", [128, NCST], F32, kind="ExternalInput").ap()
    out_d = nc.dram_tensor("outT", [D, TOK], F32, kind="ExternalOutput").ap()
    dbg_d = nc.dram_tensor("dbg", [NT, 128, DBG_COLS], F32, kind="ExternalOutput").ap() if DEBUG else None

    S = Sched()
    with ExitStack() as st:
        def sb(name, shape, dt):
            return st.enter_context(nc.sbuf_tensor("sb_" + name, shape, dt))

        par = sb("par", [128, NP_], F32)
        dpar = sb("dpar", [128, NDP], F32)
        cstf = sb("cstf", [128, NCST], F32)
        ident = sb("ident", [128, 128], BF16)
        dh = sb("dh", [128, 4 * 128], BF16)
        ones = sb("ones", [128, 128], BF16)
        ones128 = sb("ones128", [128, 128], BF16)
        ones512 = sb("ones512", [128, 128], BF16)
        ones1024 = sb("ones1024", [128, 128], BF16)
        kT = sb("kT", [128, 4 * SEQ], BF16)
        vt = sb("vt", [128, 16 * 512], BF16)
        zx = sb("zx", [128, 8 * T], F32)
        xb = sb("xb", [128, 8 * T], BF16)
        qT = sb("qT", [128, 4 * T], BF16)
        attnT = sb("attnT", [128, 4 * T], BF16)
        UW = HALO + T
        u = sb("u", [128, 4 * UW], BF16)
        diag = sb("diag", [128, 4 * CONVK * 128], BF16)
        u2T = sb("u2T", [128, 4 * T], BF16)
        mixT = sb("mixT", [128, 8 * T], BF16)
        x1b = xb
        GW = 2 + T
        graw = sb("graw", [128, 2 * GW], F32)
        ghalo = sb("ghalo", [128, NF * 2], F32)
        hidT = sb("hidT", [128, NF * T], BF16)
        wring = sb("wring", [128, RING * SLOT], BF16)
        NPT = 4
        pt = sb("pt", [128, NPT * T], BF16)
        NTMP = 8
        tmp = sb("tmp", [128, NTMP * T], F32)
        NTB = 4
        tmpb = sb("tmpb", [128, NTB * T], BF16)
        mean_sb = sb("mean_sb", [128, T], F32)
        rstd_sb = sb("rstd_sb", [128, T], F32)
        ps = [st.enter_context(nc.psum_tensor("ps%d" % i, [128, T], F32)) for i in range(8)]

        epsc = sb("epsc", [128, 1], F32)

        def rsqrt_act(out_ap, out_reg, in_ap, in_regs, mid_ap, mid_reg):
            S.op("act", lambda e: e.activation(out=mid_ap, in_=in_ap, func=AF.Ln, bias=epsc[:, 0:1]),
                 reads=list(in_regs) + ["epsc"], writes=[mid_reg])
            S.op("act", lambda e: e.activation(out=out_ap, in_=mid_ap, func=AF.Exp, scale=-0.5),
                 reads=[mid_reg], writes=[out_reg])

        def P(name, j=0, n=1):
            o, _ = _P[name]
            return par[:, o + j:o + j + n]

        def DP(name, j=0, n=1):
            o, _ = _DP[name]
            return dpar[:, o + j:o + j + n]

        rot = {"bank": 0, "tmp": 0, "tmpb": 0, "pt": 0}

        dbgc = {"n": 0}

        def dump(tile, col0, src_ap, ncols, regs):
            if not DEBUG or tile >= NT:
                return
            dbgc["n"] += 1
            S.op("pool", lambda e: e.dma_start(out=dbg_d[tile, :, col0:col0 + ncols], in_=src_ap),
                 reads=regs, writes=[("dbgd", dbgc["n"])], group="dbg%d" % dbgc["n"])

        def next_bank(nbanks=8):
            b = rot["bank"] % nbanks
            rot["bank"] += 1
            return b

        def next_tmp():
            i = rot["tmp"] % NTMP
            rot["tmp"] += 1
            return i, tmp[:, i * T:(i + 1) * T], ("tmp", i)

        def next_tmpb():
            i = rot["tmpb"] % NTB
            rot["tmpb"] += 1
            return i, tmpb[:, i * T:(i + 1) * T], ("tmpb", i)

        S.op("sp", lambda e: e.dma_start(out=par[:], in_=par_d[:, :]), writes=["par"], group="par")
        S.op("sp", lambda e: e.dma_start(out=cstf[:], in_=cst_d[:, :]), writes=["cstf"], group="cst")
        S.op("dve", lambda e: e.tensor_copy(out=ident[:], in_=cstf[:, C_IDENT:C_IDENT + 128]),
             reads=["cstf"], writes=["ident"])
        S.op("dve", lambda e: e.tensor_copy(out=dh[:], in_=cstf[:, C_DH:C_DH + 512]),
             reads=["cstf"], writes=["dh"])
        S.op("dve", lambda e: e.memset(epsc[:], LN_EPS), writes=["epsc"])
        S.op("dve", lambda e: e.memset(ones[:], 1.0), writes=["ones"])
        S.op("dve", lambda e: e.memset(ones128[:], 1.0 / 128), writes=["ones128"])
        S.op("dve", lambda e: e.memset(ones512[:], 1.0 / 512), writes=["ones512"])
        S.op("dve", lambda e: e.memset(ones1024[:], 1.0 / 1024), writes=["ones1024"])
        S.op("dve", lambda e: e.tensor_scalar_mul(out=DP("b_gate_h", 0, 16), in0=P("b_gate", 0, 16), scalar1=0.5),
             reads=["par"], writes=["dp_bg"])
        S.op("dve", lambda e: e.tensor_scalar_mul(out=DP("conv_w_h", 0, 4 * CONVK), in0=P("conv_w", 0, 4 * CONVK),
                                                  scalar1=0.5), reads=["par"], writes=["dp_cw"])
        for idx in range(4 * CONVK):
            S.op("dve", lambda e, idx=idx: e.tensor_scalar_mul(out=diag[:, idx * 128:(idx + 1) * 128], in0=ident[:],
                                                               scalar1=DP("conv_w_h", idx)),
                 reads=["ident", "dp_cw"], writes=["diag"])
        S.op("dve", lambda e: e.tensor_scalar_mul(out=DP("bc_h", 0, 8), in0=P("b_conv_proj", 0, 8), scalar1=0.5),
             reads=["par"], writes=["dp_bc"])
        S.op("dve", lambda e: e.tensor_scalar_mul(out=DP("gain08"), in0=P("gain"), scalar1=1.0 - LAMBDA_INIT),
             reads=["par"], writes=["dp_gain"])
        S.op("dve", lambda e: e.tensor_tensor(out=DP("tmp64a", 0, 64), in0=P("lq1", 0, 64), in1=P("lk1", 0, 64),
                                              op=ALU.mult), reads=["par"], writes=["dp_t64a"])
        S.op("dve", lambda e: e.tensor_tensor(out=DP("tmp64b", 0, 64), in0=P("lq2", 0, 64), in1=P("lk2", 0, 64),
                                              op=ALU.mult), reads=["par"], writes=["dp_t64b"])
        S.op("dve", lambda e: e.reduce_sum(out=DP("s1"), in_=DP("tmp64a", 0, 64), axis=mybir.AxisListType.X),
             reads=["dp_t64a"], writes=["dp_s1"])
        S.op("dve", lambda e: e.reduce_sum(out=DP("s2"), in_=DP("tmp64b", 0, 64), axis=mybir.AxisListType.X),
             reads=["dp_t64b"], writes=["dp_s2"])
        S.op("act", lambda e: e.activation(out=DP("e1"), in_=DP("s1"), func=AF.Exp), reads=["dp_s1"], writes=["dp_e1"])
        S.op("act", lambda e: e.activation(out=DP("e2"), in_=DP("s2"), func=AF.Exp), reads=["dp_s2"], writes=["dp_e2"])
        S.op("dve", lambda e: e.scalar_tensor_tensor(out=DP("neg_lam"), in0=DP("e2"), scalar=-LAMBDA_INIT, in1=DP("e1"),
                                                     op0=ALU.add, op1=ALU.subtract),
             reads=["dp_e1", "dp_e2"], writes=["dp_lam"])

        wstate = {"next_load": 0, "cur": -1}
        total_units = NSEQ * NT * NUNITS

        def emit_load(g):
            ui = g % NUNITS
            slot = g % RING
            n = UNITS[ui][1]
            off = UOFF[ui]
            S.op("pool", lambda e: e.dma_start(out=wring[:, slot * SLOT:slot * SLOT + n], in_=wst_d[:, off:off + n]),
                 writes=[("w", slot)], group="w%d" % slot)

        def use_unit(name, hold=0):
            wstate["cur"] += 1
            g = wstate["cur"]
            assert UNITS[g % NUNITS][0] == name, (UNITS[g % NUNITS][0], name)
            while wstate["next_load"] <= min(g - hold + RING - 1, total_units - 1):
                emit_load(wstate["next_load"])
                wstate["next_load"] += 1
            slot = g % RING
            return slot * SLOT, ("w", slot)

        def evac(k, out_ap, in_ap, reads, writes):
            if k % 2 == 0:
                S.op("act", lambda e: e.copy(out=out_ap, in_=in_ap), reads=reads, writes=writes)
            else:
                S.op("dve", lambda e: e.tensor_copy(out=out_ap, in_=in_ap), reads=reads, writes=writes)

        def mm(out_ap, lhsT, rhs, start, stop, reads, writes, skip=False):
            if skip:
                S.op("pe", lambda e: e.matmul(out_ap, lhsT, rhs, start=start, stop=stop, skip_group_check=True),
                     reads=reads, writes=writes)
            else:
                S.op("pe", lambda e: e.matmul(out_ap, lhsT, rhs, start=start, stop=stop), reads=reads, writes=writes)

        def proj_group(bank, wbase, wreg, ncols_unit, col0, act_t, act_reg, nk):
            for kc in range(nk):
                mm(ps[bank][:, :], wring[:, wbase + kc * ncols_unit + col0:wbase + kc * ncols_unit + col0 + 128],
                   act_t[:, kc * T:(kc + 1) * T], kc == 0, kc == nk - 1,
                   reads=[wreg, (act_reg, kc)], writes=[("ps", bank)])

        def ln_stats_and_apply(gname, bname, write_bf, final_out, tile_tok0):
            S.op("act", lambda e: e.copy(out=mean_sb[:], in_=ps[6][:, :]), reads=[("ps", 6)], writes=["mean_sb"])
            _, tq, tq_r = next_tmp()
            S.op("dve", lambda e: e.tensor_tensor(out=tq, in0=mean_sb[:], in1=mean_sb[:], op=ALU.mult),
                 reads=["mean_sb"], writes=[tq_r])
            S.op("dve", lambda e: e.tensor_tensor(out=tq, in0=ps[7][:, :], in1=tq, op=ALU.subtract),
                 reads=[("ps", 7), tq_r], writes=[tq_r])
            rsqrt_act(rstd_sb[:], "rstd_sb", tq, [tq_r], tq, tq_r)
            for dc in range(8):
                zc = zx[:, dc * T:(dc + 1) * T]
                S.op("dve", lambda e, zc=zc: e.tensor_tensor(out=zc, in0=zc, in1=mean_sb[:], op=ALU.subtract),
                     reads=[("zx", dc), "mean_sb"], writes=[("zx", dc)])
                S.op("dve", lambda e, zc=zc: e.tensor_tensor(out=zc, in0=zc, in1=rstd_sb[:], op=ALU.mult),
                     reads=[("zx", dc), "rstd_sb"], writes=[("zx", dc)])
                g_ap = P(gname, dc)
                b_ap = P(bname, dc)
                if write_bf:
                    x1c = x1b[:, dc * T:(dc + 1) * T]
                    S.op("act", lambda e, zc=zc, x1c=x1c, g_ap=g_ap, b_ap=b_ap: e.activation(
                        out=x1c, in_=zc, func=AF.Identity, bias=b_ap, scale=g_ap),
                         reads=[("zx", dc), "par"], writes=[("xb", dc)])
                S.op("act", lambda e, zc=zc, g_ap=g_ap, b_ap=b_ap: e.activation(
                    out=zc, in_=zc, func=AF.Identity, bias=b_ap, scale=g_ap),
                     reads=[("zx", dc), "par"], writes=[("zx", dc)])
                if final_out:
                    S.op("sp", lambda e, zc=zc, dc=dc: e.dma_start(
                        out=out_d[dc * 128:(dc + 1) * 128, tile_tok0:tile_tok0 + T], in_=zc),
                         reads=[("zx", dc)], writes=[("outd", dc)], group="out%d" % dc)

        def residual_and_stats(dc, bank, src_reg_extra):
            zc = zx[:, dc * T:(dc + 1) * T]
            S.op("dve", lambda e: e.scalar_tensor_tensor(out=zc, in0=zc, scalar=ALPHA, in1=ps[bank][:, :],
                                                         op0=ALU.mult, op1=ALU.add),
                 reads=[("zx", dc), ("ps", bank)], writes=[("zx", dc)])
            _, zb, zb_r = next_tmpb()
            _, zs, zs_r = next_tmpb()
            S.op("act", lambda e: e.copy(out=zb, in_=zc), reads=[("zx", dc)], writes=[zb_r])
            S.op("act", lambda e: e.activation(out=zs, in_=zc, func=AF.Square), reads=[("zx", dc)], writes=[zs_r])
            mm(ps[6][:, :], ones1024[:], zb, dc == 0, dc == 7, reads=["ones1024", zb_r], writes=[("ps", 6)])
            mm(ps[7][:, :], ones1024[:], zs, dc == 0, dc == 7, reads=["ones1024", zs_r], writes=[("ps", 7)])

        for s in range(NSEQ):
            for I in range(NT):
                tidx = s * NT + I
                tok0 = I * T
                ctok0 = s * SEQ + tok0

                S.op("sp", lambda e, tidx=tidx: e.dma_start(out=zx[:], in_=xT_d[tidx, :, :]),
                     writes=[("zx", dc) for dc in range(8)], group="xload")
                for kc in range(8):
                    evac(kc, xb[:, kc * T:(kc + 1) * T], zx[:, kc * T:(kc + 1) * T],
                         reads=[("zx", kc)], writes=[("xb", kc)])

                wb, wr = use_unit("qk0")
                for ch in range(4):
                    bank = next_bank()
                    proj_group(bank, wb, wr, 512, ch * 128, xb, "xb", 8)
                    evac(ch, qT[:, ch * T:(ch + 1) * T], ps[bank][:, :], reads=[("ps", bank)], writes=[("qT", ch)])
                wb, wr = use_unit("qk1")
                for ch in range(4):
                    bank = next_bank()
                    proj_group(bank, wb, wr, 512, ch * 128, xb, "xb", 8)
                    evac(ch + 1, kT[:, ch * SEQ + tok0:ch * SEQ + tok0 + T], ps[bank][:, :],
                         reads=[("ps", bank)], writes=[("kT", ch, I)])
                wb, wr = use_unit("v")
                for ts in range(4):
                    bank = next_bank()
                    for kc in range(8):
                        mm(ps[bank][:, :], xb[:, kc * T + ts * 128:kc * T + (ts + 1) * 128],
                           wring[:, wb + kc * 512:wb + (kc + 1) * 512], kc == 0, kc == 7,
                           reads=[wr, ("xb", kc)], writes=[("ps", bank)])
                    kt = 4 * I + ts
                    evac(ts, vt[:, kt * 512:(kt + 1) * 512], ps[bank][:, :], reads=[("ps", bank)], writes=[("vt", kt)])

                nkt = 4 * I + 4
                for h in range(HEADS):
                    pend = None
                    acc = {"O0": 4, "O1": 5, "S0": 6, "S1": 7}

                    def emit_av(job, last):
                        j, c0, ptaps = job
                        for m in range(2):
                            pap, preg = ptaps[m]
                            mm(ps[acc["O%d" % m]][:, c0:T], vt[:, j * 512 + h * 128:j * 512 + (h + 1) * 128], pap,
                               j == 0, last, reads=[("vt", j), preg], writes=[("ps", acc["O%d" % m])], skip=True)
                            mm(ps[acc["S%d" % m]][:, c0:T], ones[:], pap,
                               j == 0, last, reads=["ones", preg], writes=[("ps", acc["S%d" % m])], skip=True)

                    for j in range(nkt):
                        r = j - 4 * I
                        c0 = 0 if r < 0 else 128 * r
                        mi = (4 * I - j) + 3
                        bias_ap = cstf[:, C_BIAS + h * 16 + mi:C_BIAS + h * 16 + mi + 1]
                        ptaps = []
                        for m in range(2):
                            bank = next_bank(4)
                            p0 = 64 * m
                            mm(ps[bank][:, c0:T], kT[p0:p0 + 64, h * SEQ + j * 128:h * SEQ + (j + 1) * 128],
                               qT[p0:p0 + 64, h * T + c0:h * T + T], True, r < 0,
                               reads=[("kT", h, j // 4), ("qT", h)], writes=[("ps", bank)], skip=True)
                            if r >= 0:
                                mm(ps[bank][:, c0:c0 + 128], ident[:], dh[:, h * 128:(h + 1) * 128], False, True,
                                   reads=["ident", "dh"], writes=[("ps", bank)], skip=True)
                            pi = rot["pt"] % NPT
                            rot["pt"] += 1
                            pap = pt[:, pi * T + c0:(pi + 1) * T]
                            S.op("act", lambda e, pap=pap, bank=bank, c0=c0, bias_ap=bias_ap: e.activation(
                                out=pap, in_=ps[bank][:, c0:T], func=AF.Exp, bias=bias_ap, scale=QK_SCALE),
                                 reads=[("ps", bank), "cstf"], writes=[("pt", pi)])
                            ptaps.append((pap, ("pt", pi)))
                        if pend is not None:
                            emit_av(pend, False)
                        pend = (j, c0, ptaps)
                    emit_av(pend, True)

                    _, r1, r1_r = next_tmp()
                    _, r2, r2_r = next_tmp()
                    _, t1, t1_r = next_tmp()
                    _, t2, t2_r = next_tmp()
                    S.op("dve", lambda e, r1=r1: e.reciprocal(out=r1, in_=ps[6][:, :]), reads=[("ps", 6)], writes=[r1_r])
                    S.op("dve", lambda e, r2=r2: e.reciprocal(out=r2, in_=ps[7][:, :]), reads=[("ps", 7)], writes=[r2_r])
                    S.op("dve", lambda e, t1=t1, r1=r1: e.tensor_tensor(out=t1, in0=ps[4][:, :], in1=r1, op=ALU.mult),
                         reads=[("ps", 4), r1_r], writes=[t1_r])
                    S.op("dve", lambda e, t2=t2, r2=r2: e.tensor_tensor(out=t2, in0=ps[5][:, :], in1=r2, op=ALU.mult),
                         reads=[("ps", 5), r2_r], writes=[t2_r])
                    S.op("dve", lambda e, t1=t1, t2=t2: e.scalar_tensor_tensor(
                        out=t1, in0=t2, scalar=DP("neg_lam"), in1=t1, op0=ALU.mult, op1=ALU.add),
                         reads=[t1_r, t2_r, "dp_lam"], writes=[t1_r])
                    _, osq, osq_r = next_tmpb()
                    S.op("act", lambda e, t1=t1, osq=osq: e.activation(out=osq, in_=t1, func=AF.Square),
                         reads=[t1_r], writes=[osq_r])
                    bank = next_bank(4)
                    mm(ps[bank][:, :], ones128[:], osq, True, True, reads=["ones128", osq_r], writes=[("ps", bank)])
                    rsqrt_act(r1, r1_r, ps[bank][:, :], [("ps", bank)], r1, r1_r)
                    S.op("dve", lambda e, t1=t1, r1=r1, h=h: e.scalar_tensor_tensor(
                        out=attnT[:, h * T:(h + 1) * T], in0=t1, scalar=DP("gain08"), in1=r1,
                        op0=ALU.mult, op1=ALU.mult),
                         reads=[t1_r, r1_r, "dp_gain"], writes=[("attnT", h)])

                dump(tidx, 0, attnT[:], 2048, [("attnT", h) for h in range(4)])
                wba, wra = use_unit("glu_a")
                wbb, wrb = use_unit("glu_b", hold=1)
                for ch in range(4):
                    ub = ch * UW
                    if I == 0:
                        S.op("dve", lambda e, ub=ub: e.memset(u[:, ub:ub + HALO], 0.0), writes=[("u", ch)])
                    else:
                        S.op("dve", lambda e, ub=ub: e.tensor_copy(out=u[:, ub:ub + HALO], in_=u[:, ub + T:ub + T + HALO]),
                             reads=[("u", ch)], writes=[("u", ch)])
                    bank_a = next_bank()
                    proj_group(bank_a, wba, wra, 512, ch * 128, xb, "xb", 8)
                    bank_b = next_bank()
                    proj_group(bank_b, wbb, wrb, 512, ch * 128, xb, "xb", 8)
                    _, sg, sg_r = next_tmp()
                    S.op("act", lambda e, sg=sg, bank_b=bank_b: e.activation(out=sg, in_=ps[bank_b][:, :], func=AF.Tanh,
                                                                             scale=0.5),
                         reads=[("ps", bank_b)], writes=[sg_r])
                    S.op("dve", lambda e, sg=sg, bank_a=bank_a, ub=ub: e.scalar_tensor_tensor(
                        out=u[:, ub + HALO:ub + HALO + T], in0=sg, scalar=1.0, in1=ps[bank_a][:, :],
                        op0=ALU.add, op1=ALU.mult),
                         reads=[sg_r, ("ps", bank_a)], writes=[("u", ch)])
                cbanks = [next_bank() for _ in range(4)]
                bm = next_bank()
                bq = next_bank()
                for ch in range(4):
                    ub = ch * UW
                    cbk = cbanks[ch]
                    for k in range(CONVK):
                        di = ch * CONVK + k
                        mm(ps[cbk][:, :], diag[:, di * 128:(di + 1) * 128], u[:, ub + k:ub + k + T], k == 0, k == CONVK - 1,
                           reads=["diag", ("u", ch)], writes=[("ps", cbk)])
                    _, cb, cb_r = next_tmpb()
                    _, cs, cs_r = next_tmpb()
                    S.op("act", lambda e, cb=cb, cbk=cbk, ch=ch: e.activation(out=cb, in_=ps[cbk][:, :], func=AF.Identity,
                                                                              bias=P("conv_b", ch)),
                         reads=[("ps", cbk), "par"], writes=[cb_r])
                    S.op("act", lambda e, cs=cs, cbk=cbk, ch=ch: e.activation(out=cs, in_=ps[cbk][:, :], func=AF.Square,
                                                                              bias=P("conv_b", ch)),
                         reads=[("ps", cbk), "par"], writes=[cs_r])
                    mm(ps[bm][:, :], ones512[:], cb, ch == 0, ch == 3, reads=["ones512", cb_r], writes=[("ps", bm)])
                    mm(ps[bq][:, :], ones512[:], cs, ch == 0, ch == 3, reads=["ones512", cs_r], writes=[("ps", bq)])
                S.op("act", lambda e, bm=bm: e.copy(out=mean_sb[:], in_=ps[bm][:, :]), reads=[("ps", bm)],
                     writes=["mean_sb"])
                _, tq, tq_r = next_tmp()
                S.op("dve", lambda e, tq=tq: e.tensor_tensor(out=tq, in0=mean_sb[:], in1=mean_sb[:], op=ALU.mult),
                     reads=["mean_sb"], writes=[tq_r])
                S.op("dve", lambda e, tq=tq, bq=bq: e.tensor_tensor(out=tq, in0=ps[bq][:, :], in1=tq, op=ALU.subtract),
                     reads=[("ps", bq), tq_r], writes=[tq_r])
                rsqrt_act(rstd_sb[:], "rstd_sb", tq, [tq_r], tq, tq_r)
                for ch in range(4):
                    cbk = cbanks[ch]
                    _, tc, tc_r = next_tmp()
                    S.op("dve", lambda e, tc=tc, cbk=cbk, ch=ch: e.scalar_tensor_tensor(
                        out=tc, in0=ps[cbk][:, :], scalar=P("conv_b", ch), in1=mean_sb[:], op0=ALU.add, op1=ALU.subtract),
                         reads=[("ps", cbk), "mean_sb", "par"], writes=[tc_r])
                    S.op("dve", lambda e, tc=tc: e.tensor_tensor(out=tc, in0=tc, in1=rstd_sb[:], op=ALU.mult),
                         reads=[tc_r, "rstd_sb"], writes=[tc_r])
                    S.op("act", lambda e, tc=tc, ch=ch: e.activation(
                        out=u2T[:, ch * T:(ch + 1) * T], in_=tc, func=AF.Silu, bias=P("conv_beta", ch),
                        scale=P("conv_g", ch)),
                         reads=[tc_r, "par"], writes=[("u2T", ch)])

                dump(tidx, 2048, u2T[:], 2048, [("u2T", c) for c in range(4)])
                for dc in range(8):
                    wb, wr = use_unit("mix%d" % dc)
                    b_ga = next_bank()
                    for kc in range(8):
                        mm(ps[b_ga][:, :], wring[:, wb + kc * 128:wb + (kc + 1) * 128], xb[:, kc * T:(kc + 1) * T],
                           kc == 0, kc == 7, reads=[wr, ("xb", kc)], writes=[("ps", b_ga)])
                    b_a = next_bank()
                    for kc in range(4):
                        mm(ps[b_a][:, :], wring[:, wb + 2048 + kc * 128:wb + 2048 + (kc + 1) * 128],
                           attnT[:, kc * T:(kc + 1) * T], kc == 0, kc == 3,
                           reads=[wr, ("attnT", kc)], writes=[("ps", b_a)])
                    b_gc = next_bank()
                    for kc in range(8):
                        mm(ps[b_gc][:, :], wring[:, wb + 1024 + kc * 128:wb + 1024 + (kc + 1) * 128],
                           xb[:, kc * T:(kc + 1) * T], kc == 0, kc == 7, reads=[wr, ("xb", kc)], writes=[("ps", b_gc)])
                    b_c = next_bank()
                    for kc in range(4):
                        mm(ps[b_c][:, :], wring[:, wb + 2560 + kc * 128:wb + 2560 + (kc + 1) * 128],
                           u2T[:, kc * T:(kc + 1) * T], kc == 0, kc == 3,
                           reads=[wr, ("u2T", kc)], writes=[("ps", b_c)])
                    _, tha, tha_r = next_tmp()
                    _, thc, thc_r = next_tmp()
                    _, ah, ah_r = next_tmp()
                    _, ch_, ch_r = next_tmp()
                    S.op("act", lambda e, tha=tha, b_ga=b_ga, dc=dc: e.activation(
                        out=tha, in_=ps[b_ga][:, :], func=AF.Tanh, bias=DP("b_gate_h", dc), scale=0.5),
                         reads=[("ps", b_ga), "dp_bg"], writes=[tha_r])
                    S.op("act", lambda e, ah=ah, b_a=b_a: e.activation(out=ah, in_=ps[b_a][:, :], func=AF.Identity, scale=0.5),
                         reads=[("ps", b_a)], writes=[ah_r])
                    S.op("act", lambda e, thc=thc, b_gc=b_gc, dc=dc: e.activation(
                        out=thc, in_=ps[b_gc][:, :], func=AF.Tanh, bias=DP("b_gate_h", 8 + dc), scale=0.5),
                         reads=[("ps", b_gc), "dp_bg"], writes=[thc_r])
                    S.op("act", lambda e, ch_=ch_, b_c=b_c, dc=dc: e.activation(
                        out=ch_, in_=ps[b_c][:, :], func=AF.Identity, bias=DP("bc_h", dc), scale=0.5),
                         reads=[("ps", b_c), "dp_bc"], writes=[ch_r])
                    S.op("dve", lambda e, tha=tha, ah=ah: e.scalar_tensor_tensor(
                        out=ah, in0=tha, scalar=1.0, in1=ah, op0=ALU.add, op1=ALU.mult),
                         reads=[tha_r, ah_r], writes=[ah_r])
                    S.op("dve", lambda e, thc=thc, ch_=ch_: e.scalar_tensor_tensor(
                        out=ch_, in0=thc, scalar=1.0, in1=ch_, op0=ALU.add, op1=ALU.mult),
                         reads=[thc_r, ch_r], writes=[ch_r])
                    S.op("dve", lambda e, ah=ah, ch_=ch_, dc=dc: e.tensor_tensor(
                        out=mixT[:, dc * T:(dc + 1) * T], in0=ah, in1=ch_, op=ALU.add),
                         reads=[ah_r, ch_r], writes=[("mixT", dc)])

                dump(tidx, 4096, mixT[:], 4096, [("mixT", c) for c in range(8)])
                for half in range(2):
                    wb, wr = use_unit("wo%d" % half)
                    for c4 in range(4):
                        dc = half * 4 + c4
                        bank = next_bank(6)
                        proj_group(bank, wb, wr, 512, c4 * 128, mixT, "mixT", 8)
                        residual_and_stats(dc, bank, None)
                ln_stats_and_apply("ln1_g", "ln1_b", True, False, ctok0)
                dump(tidx, 8192, zx[:], 4096, [("zx", c) for c in range(8)])

                for g in range(6):
                    nch = 4 if g < 5 else 2
                    wbg, wrg = use_unit("fg%d" % g)
                    wbv, wrv = use_unit("fv%d" % g, hold=1)
                    ncu = nch * 128
                    for c in range(nch):
                        f = g * 4 + c
                        slot = f % 2
                        gb = slot * GW
                        bank_g = next_bank()
                        proj_group(bank_g, wbg, wrg, ncu, c * 128, x1b, "xb", 8)
                        bank_v = next_bank()
                        proj_group(bank_v, wbv, wrv, ncu, c * 128, x1b, "xb", 8)
                        if I == 0:
                            S.op("dve", lambda e, gb=gb: e.memset(graw[:, gb:gb + 2], 0.0),
                                 writes=[("graw", slot)])
                        else:
                            S.op("act", lambda e, gb=gb, f=f: e.copy(out=graw[:, gb:gb + 2], in_=ghalo[:, f * 2:f * 2 + 2]),
                                 reads=[("ghalo", f)], writes=[("graw", slot)])
                        S.op("act", lambda e, gb=gb, bank_g=bank_g: e.copy(out=graw[:, gb + 2:gb + 2 + T],
                                                                         in_=ps[bank_g][:, :]),
                             reads=[("ps", bank_g)], writes=[("graw", slot)])
                        S.op("act", lambda e, gb=gb, f=f: e.copy(out=ghalo[:, f * 2:f * 2 + 2], in_=graw[:, gb + T:gb + T + 2]),
                             reads=[("graw", slot)], writes=[("ghalo", f)])
                        _, gc_, gc_r = next_tmp()
                        S.op("act", lambda e, gc_=gc_, bank_g=bank_g, f=f: e.activation(
                            out=gc_, in_=ps[bank_g][:, :], func=AF.Identity, bias=P("ffn_b", f), scale=P("ffn_w", f * 3 + 2)),
                             reads=[("ps", bank_g), "par"], writes=[gc_r])
                        S.op("dve", lambda e, gc_=gc_, gb=gb, f=f: e.scalar_tensor_tensor(
                            out=gc_, in0=graw[:, gb + 1:gb + 1 + T], scalar=P("ffn_w", f * 3 + 1), in1=gc_,
                            op0=ALU.mult, op1=ALU.add),
                             reads=[("graw", slot), gc_r, "par"], writes=[gc_r])
                        S.op("dve", lambda e, gc_=gc_, gb=gb, f=f: e.scalar_tensor_tensor(
                            out=gc_, in0=graw[:, gb:gb + T], scalar=P("ffn_w", f * 3 + 0), in1=gc_,
                            op0=ALU.mult, op1=ALU.add),
                             reads=[("graw", slot), gc_r, "par"], writes=[gc_r])
                        S.op("act", lambda e, gc_=gc_: e.activation(out=gc_, in_=gc_, func=AF.Gelu),
                             reads=[gc_r], writes=[gc_r])
                        S.op("dve", lambda e, gc_=gc_, bank_v=bank_v, f=f: e.tensor_tensor(
                            out=hidT[:, f * T:(f + 1) * T], in0=gc_, in1=ps[bank_v][:, :], op=ALU.mult),
                             reads=[gc_r, ("ps", bank_v)], writes=[("hidT", f)])

                for dc in range(8):
                    wb, wr = use_unit("dn%d" % dc)
                    bank = next_bank(6)
                    proj_group(bank, wb, wr, 128, 0, hidT, "hidT", NF)
                    residual_and_stats(dc, bank, None)
                ln_stats_and_apply("ln2_g", "ln2_b", False, True, ctok0)

        S.op("sp", None, reads=[("outd", dc) for dc in range(8)])
        if DEBUG:
            S.op("pool", None, reads=[("dbgd", i + 1) for i in range(dbgc["n"])])
        S.emit(nc, st)
    return nc


_NC_CACHE = {}
DEBUG = False
DBG_COLS = 2048 + 2048 + 4096 + 4096
_DBG = {}


def kernel(**inputs):
    x = np.asarray(inputs["x"], np.float32)
    wst = _pack_weights(*[np.asarray(inputs[k], np.float32) for k in
                          ("w_in", "w_attn_proj", "w_conv_proj", "w_out", "w_ffn_in", "w_ffn_down")])
    par = _pack_params({k: np.asarray(v, np.float32) for k, v in inputs.items()})
    cst = _make_consts()
    if "nc" not in _NC_CACHE:
        _NC_CACHE["nc"] = build_nc()
    nc = _NC_CACHE["nc"]
    in_maps = []
    for c in range(NCORES):
        xs = x[NSEQ * c:NSEQ * (c + 1)]
        xt = xs.reshape(NSEQ, NT, T, 8, 128).transpose(0, 1, 4, 3, 2)
        xt = np.ascontiguousarray(xt).reshape(NSEQ * NT, 128, 8 * T)
        in_maps.append({"xT": xt, "wst": wst, "par": par, "cst": cst})
    res = run_bass_kernel_spmd(nc, in_maps, core_ids=list(range(NCORES)))
    if DEBUG:
        _DBG["dbg"] = np.asarray(res.results[0]["dbg"])
    out = np.empty((16, SEQ, D), np.float32)
    for c in range(NCORES):
        oT = np.asarray(res.results[c]["outT"])
        out[NSEQ * c:NSEQ * (c + 1)] = oT.T.reshape(NSEQ, SEQ, D)
    return out
```

```python
import math
from contextlib import ExitStack

import numpy as np
import concourse.bass as bass
import concourse.mybir as mybir
from concourse.bass_utils import run_bass_kernel_spmd

F32 = mybir.dt.float32
BF16 = mybir.dt.bfloat16
ALU = mybir.AluOpType
AF = mybir.ActivationFunctionType

NCORES = 8
D = 1024
SEQ = 2048
T = 512
NT = SEQ // T
NSEQ = 2
TOK = NSEQ * SEQ
FF = 2816
NF = FF // 128
HEADS = 4
CONVK = 31
HALO = CONVK - 1
LN_EPS = 1e-5
ALPHA = 2.0 ** 0.25
LAMBDA_INIT = 0.8 - 0.6 * math.exp(0.0)
QK_SCALE = 0.125
NEG = -30000.0
SLOPES = [2.0 ** (-8.0 * (h + 1) / HEADS) for h in range(HEADS)]

_P = {}
_off = 0
for _name, _n in [("b_gate", 16), ("conv_w", 4 * CONVK), ("conv_b", 4), ("conv_g", 4), ("conv_beta", 4),
                  ("b_conv_proj", 8), ("ln1_g", 8), ("ln1_b", 8), ("ln2_g", 8), ("ln2_b", 8),
                  ("ffn_w", NF * 3), ("ffn_b", NF), ("gain", 1),
                  ("lq1", 64), ("lk1", 64), ("lq2", 64), ("lk2", 64)]:
    _P[_name] = (_off, _n)
    _off += _n
NP_ = _off

_DP = {}
_off = 0
for _name, _n in [("b_gate_h", 16), ("conv_w_h", 4 * CONVK), ("bc_h", 8), ("gain08", 1), ("neg_lam", 1),
                  ("tmp64a", 64), ("tmp64b", 64), ("s1", 1), ("s2", 1), ("e1", 1), ("e2", 1)]:
    _DP[_name] = (_off, _n)
    _off += _n
NDP = _off

C_IDENT = 0
C_DH = 128
C_BIAS = 128 + 4 * 128
NCST = C_BIAS + 64

UNITS = []
UNITS += [("qk0", 4096), ("qk1", 4096), ("v", 4096), ("glu_a", 4096), ("glu_b", 4096)]
UNITS += [("mix%d" % dc, 3072) for dc in range(8)]
UNITS += [("wo0", 4096), ("wo1", 4096)]
for g in range(6):
    nch = 4 if g < 5 else 2
    UNITS += [("fg%d" % g, 8 * 128 * nch), ("fv%d" % g, 8 * 128 * nch)]
UNITS += [("dn%d" % dc, NF * 128) for dc in range(8)]
NUNITS = len(UNITS)
UOFF = []
_o = 0
for _n, _sz in UNITS:
    UOFF.append(_o)
    _o += _sz
WTOT = _o
SLOT = 4096
RING = 4


def _pack_weights(w_in, w_attn_proj, w_conv_proj, w_out, w_ffn_in, w_ffn_down):
    w_in = w_in[0]; wa = w_attn_proj[0]; wc = w_conv_proj[0]; wo = w_out[0]
    wf = w_ffn_in[0]; wd = w_ffn_down[0]
    out = np.empty((128, WTOT), np.float32)

    def kmaj(w, c0, c1):
        k = w.shape[0] // 128
        blk = w[:, c0:c1].reshape(k, 128, c1 - c0)
        return np.ascontiguousarray(blk.transpose(1, 0, 2)).reshape(128, k * (c1 - c0))

    blocks = {}
    blocks["qk0"] = kmaj(w_in, 0, 512)
    blocks["qk1"] = kmaj(w_in, 512, 1024)
    blocks["v"] = kmaj(w_in, 1024, 1536)
    blocks["glu_a"] = kmaj(w_in, 1536, 2048)
    blocks["glu_b"] = kmaj(w_in, 2048, 2560)
    for dc in range(8):
        ga = kmaj(w_in, 2560 + dc * 128, 2560 + (dc + 1) * 128)
        gc = kmaj(w_in, 3584 + dc * 128, 3584 + (dc + 1) * 128)
        a = kmaj(wa, dc * 128, (dc + 1) * 128)
        c = kmaj(wc, dc * 128, (dc + 1) * 128)
        blocks["mix%d" % dc] = np.concatenate([ga, gc, a, c], axis=1)
    blocks["wo0"] = kmaj(wo, 0, 512)
    blocks["wo1"] = kmaj(wo, 512, 1024)
    for g in range(6):
        nch = 4 if g < 5 else 2
        blocks["fg%d" % g] = kmaj(wf, g * 512, g * 512 + nch * 128)
        blocks["fv%d" % g] = kmaj(wf, FF + g * 512, FF + g * 512 + nch * 128)
    for dc in range(8):
        blocks["dn%d" % dc] = kmaj(wd, dc * 128, (dc + 1) * 128)
    for (name, sz), off in zip(UNITS, UOFF):
        b = blocks[name]
        assert b.shape == (128, sz), (name, b.shape, sz)
        out[:, off:off + sz] = b
    return out


def _pack_params(inp):
    par = np.zeros((128, NP_), np.float32)

    def put(name, arr):
        o, n = _P[name]
        assert arr.shape == (128, n), (name, arr.shape, n)
        par[:, o:o + n] = arr

    def cols(v):
        return np.ascontiguousarray(v.reshape(-1, 128).T)

    put("b_gate", cols(inp["b_gate"][0]))
    cw = inp["conv_dw_w"][0]
    put("conv_w", np.ascontiguousarray(cw.reshape(CONVK, 4, 128).transpose(2, 1, 0)).reshape(128, 4 * CONVK))
    put("conv_b", cols(inp["conv_dw_b"][0]))
    put("conv_g", cols(inp["conv_ln_g"][0]))
    put("conv_beta", cols(inp["conv_ln_b"][0]))
    put("b_conv_proj", cols(inp["b_conv_proj"][0]))
    put("ln1_g", cols(inp["ln1_g"][0])); put("ln1_b", cols(inp["ln1_b"][0]))
    put("ln2_g", cols(inp["ln2_g"][0])); put("ln2_b", cols(inp["ln2_b"][0]))
    fw = inp["ffn_dw_w"][0]
    put("ffn_w", np.ascontiguousarray(fw.reshape(3, NF, 128).transpose(2, 1, 0)).reshape(128, NF * 3))
    put("ffn_b", cols(inp["ffn_dw_b"][0]))
    put("gain", inp["subln_gain"][0].reshape(128, 1))
    for nm, key in [("lq1", "lambda_q1"), ("lk1", "lambda_k1"), ("lq2", "lambda_q2"), ("lk2", "lambda_k2")]:
        put(nm, np.broadcast_to(inp[key][0][None, :], (128, 64)))
    return par


def _make_consts():
    cst = np.zeros((128, NCST), np.float32)
    cst[:, C_IDENT:C_IDENT + 128] = np.eye(128, dtype=np.float32)
    kr = np.arange(128)[:, None].astype(np.float64)
    qr = np.arange(128)[None, :].astype(np.float64)
    for h in range(HEADS):
        sl = SLOPES[h]
        corr = 2.0 * sl * np.minimum(qr - kr, 0.0)
        mask = np.where((kr >= 64) & (qr < 64), NEG, 0.0)
        cst[:, C_DH + h * 128:C_DH + (h + 1) * 128] = (corr + mask) / QK_SCALE
        for mi in range(16):
            m = mi - 3
            cst[:, C_BIAS + h * 16 + mi] = sl * (np.arange(128) - 256.0 - 128.0 * m)
    return cst


class _Op:
    __slots__ = ("eng", "fn", "deps", "signal", "value", "group", "idx")

    def __init__(self, eng, fn, group, idx):
        self.eng = eng; self.fn = fn; self.deps = []; self.signal = False
        self.value = None; self.group = group; self.idx = idx


class Sched:
    ENGS = ("pe", "act", "dve", "pool", "sp")

    def __init__(self):
        self.ops = []
        self.last_w = {}
        self.readers = {}

    def op(self, eng, fn, reads=(), writes=(), group=None):
        idx = len(self.ops)
        o = _Op(eng, fn, group, idx)
        deps = set()
        for r in reads:
            if r in self.last_w:
                deps.add(self.last_w[r])
        for w in writes:
            if w in self.last_w:
                deps.add(self.last_w[w])
            for ri in self.readers.get(w, {}).values():
                deps.add(ri)
        for d in sorted(deps):
            od = self.ops[d]
            if od.group is None and od.eng == eng and eng == "pe" and group is None:
                continue
            o.deps.append(d)
            od.signal = True
        rkey = eng if group is None else ("dma", group)
        for r in reads:
            self.readers.setdefault(r, {})[rkey] = idx
        for w in writes:
            self.last_w[w] = idx
            self.readers[w] = {}
        self.ops.append(o)
        return o

    def emit(self, nc, stack):
        cnt = {}
        for o in self.ops:
            if o.group is not None:
                k = ("g", o.group)
                cnt[k] = cnt.get(k, 0) + 16
                o.value = (k, cnt[k])
            elif o.signal:
                k = ("e", o.eng)
                cnt[k] = cnt.get(k, 0) + 1
                o.value = (k, cnt[k])
        sems = {}
        for k in cnt:
            nm = "s_" + "_".join(str(x) for x in k)
            sems[k] = stack.enter_context(nc.semaphore(nm))
        per_eng = {e: [o for o in self.ops if o.eng == e] for e in self.ENGS}
        ops = self.ops

        def run(engobj, lst):
            known = {}
            for o in lst:
                need = {}
                for d in o.deps:
                    k, v = ops[d].value
                    if need.get(k, 0) < v:
                        need[k] = v
                for k, v in need.items():
                    if known.get(k, 0) < v:
                        engobj.wait_ge(sems[k], v)
                        known[k] = v
                if o.fn is None:
                    continue
                ins = o.fn(engobj)
                if o.value is not None:
                    k, v = o.value
                    ins.then_inc(sems[k], 16 if o.group is not None else 1)

        block = stack.enter_context(nc.Block())

        @block.sync
        def _(e):
            run(e, per_eng["sp"])

        @block.gpsimd
        def _(e):
            run(e, per_eng["pool"])

        @block.scalar
        def _(e):
            run(e, per_eng["act"])

        @block.vector
        def _(e):
            run(e, per_eng["dve"])

        @block.tensor
        def _(e):
            run(e, per_eng["pe"])


def build_nc():
    nc = bass.Bass("TRN2", target_bir_lowering=False)
    xT_d = nc.dram_tensor("xT", [NSEQ * NT, 128, 8 * T], F32, kind="ExternalInput").ap()
    wst_d = nc.dram_tensor("wst", [128, WTOT], F32, kind="ExternalInput").ap()
    par_d = nc.dram_tensor("par", [128, NP_], F32, kind="ExternalInput").ap()
    cst_d = nc.dram_tensor("cst", [128, NCST], F32, kind="ExternalInput").ap()
    out_d = nc.dram_tensor("outT", [D, TOK], F32, kind="ExternalOutput").ap()
    dbg_d = nc.dram_tensor("dbg", [NT, 128, DBG_COLS], F32, kind="ExternalOutput").ap() if DEBUG else None

    S = Sched()
    with ExitStack() as st:
        def sb(name, shape, dt):
            return st.enter_context(nc.sbuf_tensor("sb_" + name, shape, dt))

        par = sb("par", [128, NP_], F32)
        dpar = sb("dpar", [128, NDP], F32)
        cstf = sb("cstf", [128, 64], F32)
        ident = sb("ident", [128, 128], BF16)
        dh = sb("dh", [128, 4 * 128], BF16)
        ones = sb("ones", [128, 128], BF16)
        ones128 = sb("ones128", [128, 128], BF16)
        ones512 = sb("ones512", [128, 128], BF16)
        ones1024 = sb("ones1024", [128, 128], BF16)
        kT = sb("kT", [128, 4 * SEQ], BF16)
        vt = sb("vt", [128, 16 * 512], BF16)
        zx = sb("zx", [128, 8 * T], F32)
        xb = sb("xb", [128, 8 * T], BF16)
        xst = sb("xst", [128, 2 * T], F32)
        qT = sb("qT", [128, 4 * T], BF16)
        attnT = sb("attnT", [128, 4 * T], BF16)
        UW = HALO + T
        u = sb("u", [128, 4 * UW], BF16)
        diag = sb("diag", [128, 4 * CONVK * 128], BF16)
        u2T = sb("u2T", [128, 4 * T], BF16)
        mixT = sb("mixT", [128, 8 * T], BF16)
        x1b = xb
        GW = 2 + T
        graw = sb("graw", [128, 2 * GW], F32)
        ghalo = sb("ghalo", [128, NF * 2], F32)
        hidT = sb("hidT", [128, NF * T], BF16)
        wring = sb("wring", [128, RING * SLOT], BF16)
        NPT = 4
        pt = sb("pt", [128, NPT * T], BF16)
        NTMP = 8
        tmp = sb("tmp", [128, NTMP * T], F32)
        NTB = 4
        tmpb = sb("tmpb", [128, NTB * T], BF16)
        mean_sb = sb("mean_sb", [128, T], F32)
        rstd_sb = sb("rstd_sb", [128, T], F32)
        ps = [st.enter_context(nc.psum_tensor("ps%d" % i, [128, T], F32)) for i in range(8)]

        epsc = sb("epsc", [128, 1], F32)

        def rsqrt_act(out_ap, out_reg, in_ap, in_regs, mid_ap, mid_reg):
            S.op("act", lambda e: e.activation(out=mid_ap, in_=in_ap, func=AF.Ln, bias=epsc[:, 0:1]),
                 reads=list(in_regs) + ["epsc"], writes=[mid_reg])
            S.op("act", lambda e: e.activation(out=out_ap, in_=mid_ap, func=AF.Exp, scale=-0.5),
                 reads=[mid_reg], writes=[out_reg])

        def P(name, j=0, n=1):
            o, _ = _P[name]
            return par[:, o + j:o + j + n]

        def DP(name, j=0, n=1):
            o, _ = _DP[name]
            return dpar[:, o + j:o + j + n]

        rot = {"bank": 0, "tmp": 0, "tmpb": 0, "pt": 0}

        dbgc = {"n": 0}

        def dump(tile, col0, src_ap, ncols, regs):
            if not DEBUG or tile >= NT:
                return
            dbgc["n"] += 1
            S.op("pool", lambda e: e.dma_start(out=dbg_d[tile, :, col0:col0 + ncols], in_=src_ap),
                 reads=regs, writes=[("dbgd", dbgc["n"])], group="dbg%d" % dbgc["n"])

        def next_bank(nbanks=8):
            b = rot["bank"] % nbanks
            rot["bank"] += 1
            return b

        def next_tmp():
            i = rot["tmp"] % NTMP
            rot["tmp"] += 1
            return i, tmp[:, i * T:(i + 1) * T], ("tmp", i)

        def next_tmpb():
            i = rot["tmpb"] % NTB
            rot["tmpb"] += 1
            return i, tmpb[:, i * T:(i + 1) * T], ("tmpb", i)

        S.op("sp", lambda e: e.dma_start(out=par[:], in_=par_d[:, :]), writes=["par"], group="par")
        S.op("sp", lambda e: e.dma_start(out=cstf[:], in_=cst_d[:, C_BIAS:C_BIAS + 64]), writes=["cstf"], group="cst")
        S.op("pool", lambda e: e.dma_start(out=ident[:], in_=cst_d[:, C_IDENT:C_IDENT + 128]), writes=["ident"],
             group="cst_i")
        S.op("pool", lambda e: e.dma_start(out=dh[:], in_=cst_d[:, C_DH:C_DH + 512]), writes=["dh"], group="cst_d")
        S.op("dve", lambda e: e.memset(epsc[:], LN_EPS), writes=["epsc"])
        S.op("dve", lambda e: e.memset(ones[:], 1.0), writes=["ones"])
        S.op("dve", lambda e: e.memset(ones128[:], 1.0 / 128), writes=["ones128"])
        S.op("dve", lambda e: e.memset(ones512[:], 1.0 / 512), writes=["ones512"])
        S.op("dve", lambda e: e.memset(ones1024[:], 1.0 / 1024), writes=["ones1024"])
        S.op("dve", lambda e: e.tensor_scalar_mul(out=DP("b_gate_h", 0, 16), in0=P("b_gate", 0, 16), scalar1=0.5),
             reads=["par"], writes=["dp_bg"])
        S.op("dve", lambda e: e.tensor_scalar_mul(out=DP("conv_w_h", 0, 4 * CONVK), in0=P("conv_w", 0, 4 * CONVK),
                                                  scalar1=0.5), reads=["par"], writes=["dp_cw"])
        for idx in range(4 * CONVK):
            S.op("dve", lambda e, idx=idx: e.tensor_scalar_mul(out=diag[:, idx * 128:(idx + 1) * 128], in0=ident[:],
                                                               scalar1=DP("conv_w_h", idx)),
                 reads=["ident", "dp_cw"], writes=["diag"])
        S.op("dve", lambda e: e.tensor_scalar_mul(out=DP("bc_h", 0, 8), in0=P("b_conv_proj", 0, 8), scalar1=0.5),
             reads=["par"], writes=["dp_bc"])
        S.op("dve", lambda e: e.tensor_scalar_mul(out=DP("gain08"), in0=P("gain"), scalar1=1.0 - LAMBDA_INIT),
             reads=["par"], writes=["dp_gain"])
        S.op("dve", lambda e: e.tensor_tensor(out=DP("tmp64a", 0, 64), in0=P("lq1", 0, 64), in1=P("lk1", 0, 64),
                                              op=ALU.mult), reads=["par"], writes=["dp_t64a"])
        S.op("dve", lambda e: e.tensor_tensor(out=DP("tmp64b", 0, 64), in0=P("lq2", 0, 64), in1=P("lk2", 0, 64),
                                              op=ALU.mult), reads=["par"], writes=["dp_t64b"])
        S.op("dve", lambda e: e.reduce_sum(out=DP("s1"), in_=DP("tmp64a", 0, 64), axis=mybir.AxisListType.X),
             reads=["dp_t64a"], writes=["dp_s1"])
        S.op("dve", lambda e: e.reduce_sum(out=DP("s2"), in_=DP("tmp64b", 0, 64), axis=mybir.AxisListType.X),
             reads=["dp_t64b"], writes=["dp_s2"])
        S.op("act", lambda e: e.activation(out=DP("e1"), in_=DP("s1"), func=AF.Exp), reads=["dp_s1"], writes=["dp_e1"])
        S.op("act", lambda e: e.activation(out=DP("e2"), in_=DP("s2"), func=AF.Exp), reads=["dp_s2"], writes=["dp_e2"])
        S.op("dve", lambda e: e.scalar_tensor_tensor(out=DP("neg_lam"), in0=DP("e2"), scalar=-LAMBDA_INIT, in1=DP("e1"),
                                                     op0=ALU.add, op1=ALU.subtract),
             reads=["dp_e1", "dp_e2"], writes=["dp_lam"])

        wstate = {"next_load": 0, "cur": -1}
        total_units = NSEQ * NT * NUNITS

        def emit_load(g):
            ui = g % NUNITS
            slot = g % RING
            n = UNITS[ui][1]
            off = UOFF[ui]
            S.op("pool", lambda e: e.dma_start(out=wring[:, slot * SLOT:slot * SLOT + n], in_=wst_d[:, off:off + n]),
                 writes=[("w", slot)], group="w%d" % slot)

        def use_unit(name, hold=0):
            wstate["cur"] += 1
            g = wstate["cur"]
            assert UNITS[g % NUNITS][0] == name, (UNITS[g % NUNITS][0], name)
            while wstate["next_load"] <= min(g - hold + RING - 1, total_units - 1):
                emit_load(wstate["next_load"])
                wstate["next_load"] += 1
            slot = g % RING
            return slot * SLOT, ("w", slot)

        def xload(t):
            S.op("pool", lambda e: e.dma_start(out=xb[:], in_=xT_d[t, :, :]),
                 writes=[("xb", kc) for kc in range(8)], group="xload")

        def evac(k, out_ap, in_ap, reads, writes):
            if k % 2 == 0:
                S.op("act", lambda e: e.copy(out=out_ap, in_=in_ap), reads=reads, writes=writes)
            else:
                S.op("dve", lambda e: e.tensor_copy(out=out_ap, in_=in_ap), reads=reads, writes=writes)

        def mm(out_ap, lhsT, rhs, start, stop, reads, writes, skip=False):
            if skip:
                S.op("pe", lambda e: e.matmul(out_ap, lhsT, rhs, start=start, stop=stop, skip_group_check=True),
                     reads=reads, writes=writes)
            else:
                S.op("pe", lambda e: e.matmul(out_ap, lhsT, rhs, start=start, stop=stop), reads=reads, writes=writes)

        def proj_group(bank, wbase, wreg, ncols_unit, col0, act_t, act_reg, nk):
            for kc in range(nk):
                mm(ps[bank][:, :], wring[:, wbase + kc * ncols_unit + col0:wbase + kc * ncols_unit + col0 + 128],
                   act_t[:, kc * T:(kc + 1) * T], kc == 0, kc == nk - 1,
                   reads=[wreg, (act_reg, kc)], writes=[("ps", bank)])

        def ln_stats_and_apply(gname, bname, write_bf, final_out, tile_tok0):
            S.op("act", lambda e: e.copy(out=mean_sb[:], in_=ps[6][:, :]), reads=[("ps", 6)], writes=["mean_sb"])
            _, tq, tq_r = next_tmp()
            S.op("dve", lambda e: e.tensor_tensor(out=tq, in0=mean_sb[:], in1=mean_sb[:], op=ALU.mult),
                 reads=["mean_sb"], writes=[tq_r])
            S.op("dve", lambda e: e.tensor_tensor(out=tq, in0=ps[7][:, :], in1=tq, op=ALU.subtract),
                 reads=[("ps", 7), tq_r], writes=[tq_r])
            rsqrt_act(rstd_sb[:], "rstd_sb", tq, [tq_r], tq, tq_r)
            for dc in range(8):
                zc = zx[:, dc * T:(dc + 1) * T]
                S.op("dve", lambda e, zc=zc: e.tensor_tensor(out=zc, in0=zc, in1=mean_sb[:], op=ALU.subtract),
                     reads=[("zx", dc), "mean_sb"], writes=[("zx", dc)])
                S.op("dve", lambda e, zc=zc: e.tensor_tensor(out=zc, in0=zc, in1=rstd_sb[:], op=ALU.mult),
                     reads=[("zx", dc), "rstd_sb"], writes=[("zx", dc)])
                g_ap = P(gname, dc)
                b_ap = P(bname, dc)
                if write_bf:
                    x1c = x1b[:, dc * T:(dc + 1) * T]
                    S.op("act", lambda e, zc=zc, x1c=x1c, g_ap=g_ap, b_ap=b_ap: e.activation(
                        out=x1c, in_=zc, func=AF.Identity, bias=b_ap, scale=g_ap),
                         reads=[("zx", dc), "par"], writes=[("xb", dc)])
                S.op("act", lambda e, zc=zc, g_ap=g_ap, b_ap=b_ap: e.activation(
                    out=zc, in_=zc, func=AF.Identity, bias=b_ap, scale=g_ap),
                     reads=[("zx", dc), "par"], writes=[("zx", dc)])
                if final_out:
                    S.op("sp", lambda e, zc=zc, dc=dc: e.dma_start(
                        out=out_d[dc * 128:(dc + 1) * 128, tile_tok0:tile_tok0 + T], in_=zc),
                         reads=[("zx", dc)], writes=[("outd", dc)], group="out%d" % dc)

        def residual_and_stats(dc, bank, src_ap=None, src_reg=None):
            zc = zx[:, dc * T:(dc + 1) * T]
            if src_ap is None:
                src_ap, src_reg = zc, ("zx", dc)
            S.op("dve", lambda e: e.scalar_tensor_tensor(out=zc, in0=src_ap, scalar=ALPHA, in1=ps[bank][:, :],
                                                         op0=ALU.mult, op1=ALU.add),
                 reads=[src_reg, ("ps", bank)], writes=[("zx", dc)])
            _, zb, zb_r = next_tmpb()
            _, zs, zs_r = next_tmpb()
            S.op("act", lambda e: e.copy(out=zb, in_=zc), reads=[("zx", dc)], writes=[zb_r])
            S.op("act", lambda e: e.activation(out=zs, in_=zc, func=AF.Square), reads=[("zx", dc)], writes=[zs_r])
            mm(ps[6][:, :], ones1024[:], zb, dc == 0, dc == 7, reads=["ones1024", zb_r], writes=[("ps", 6)])
            mm(ps[7][:, :], ones1024[:], zs, dc == 0, dc == 7, reads=["ones1024", zs_r], writes=[("ps", 7)])

        for s in range(NSEQ):
            for I in range(NT):
                tidx = s * NT + I
                tok0 = I * T
                ctok0 = s * SEQ + tok0

                if tidx == 0:
                    xload(0)
                xst_state = {"n": 0}

                def xst_load(dc, tidx=tidx, xst_state=xst_state):
                    slot = dc % 2
                    S.op("sp", lambda e: e.dma_start(out=xst[:, slot * T:(slot + 1) * T],
                                                     in_=xT_d[tidx, :, dc * T:(dc + 1) * T]),
                         writes=[("xst", slot)], group="xst%d" % slot)
                xst_load(0)
                xst_load(1)

                wb, wr = use_unit("qk0")
                for ch in range(4):
                    bank = next_bank()
                    proj_group(bank, wb, wr, 512, ch * 128, xb, "xb", 8)
                    evac(ch, qT[:, ch * T:(ch + 1) * T], ps[bank][:, :], reads=[("ps", bank)], writes=[("qT", ch)])
                wb, wr = use_unit("qk1")
                for ch in range(4):
                    bank = next_bank()
                    proj_group(bank, wb, wr, 512, ch * 128, xb, "xb", 8)
                    evac(ch + 1, kT[:, ch * SEQ + tok0:ch * SEQ + tok0 + T], ps[bank][:, :],
                         reads=[("ps", bank)], writes=[("kT", ch, I)])
                wb, wr = use_unit("v")
                for ts in range(4):
                    bank = next_bank()
                    for kc in range(8):
                        mm(ps[bank][:, :], xb[:, kc * T + ts * 128:kc * T + (ts + 1) * 128],
                           wring[:, wb + kc * 512:wb + (kc + 1) * 512], kc == 0, kc == 7,
                           reads=[wr, ("xb", kc)], writes=[("ps", bank)])
                    kt = 4 * I + ts
                    evac(ts, vt[:, kt * 512:(kt + 1) * 512], ps[bank][:, :], reads=[("ps", bank)], writes=[("vt", kt)])

                nkt = 4 * I + 4
                deferred = []
                for h in range(HEADS):
                    pend = None
                    acc = {"O0": 4, "O1": 5, "S0": 6, "S1": 7}

                    def emit_av(job, last):
                        j, c0, ptaps = job
                        for m in range(2):
                            pap, preg = ptaps[m]
                            mm(ps[acc["O%d" % m]][:, c0:T], vt[:, j * 512 + h * 128:j * 512 + (h + 1) * 128], pap,
                               j == 0, last, reads=[("vt", j), preg], writes=[("ps", acc["O%d" % m])], skip=True)
                            mm(ps[acc["S%d" % m]][:, c0:T], ones[:], pap,
                               j == 0, last, reads=["ones", preg], writes=[("ps", acc["S%d" % m])], skip=True)

                    for j in range(nkt):
                        r = j - 4 * I
                        c0 = 0 if r < 0 else 128 * r
                        mi = (4 * I - j) + 3
                        bias_ap = cstf[:, h * 16 + mi:h * 16 + mi + 1]
                        ptaps = []
                        for m in range(2):
                            bank = next_bank(4)
                            p0 = 64 * m
                            mm(ps[bank][:, c0:T], kT[p0:p0 + 64, h * SEQ + j * 128:h * SEQ + (j + 1) * 128],
                               qT[p0:p0 + 64, h * T + c0:h * T + T], True, r < 0,
                               reads=[("kT", h, j // 4), ("qT", h)], writes=[("ps", bank)], skip=True)
                            if r >= 0:
                                mm(ps[bank][:, c0:c0 + 128], ident[:], dh[:, h * 128:(h + 1) * 128], False, True,
                                   reads=["ident", "dh"], writes=[("ps", bank)], skip=True)
                            pi = rot["pt"] % NPT
                            rot["pt"] += 1
                            pap = pt[:, pi * T + c0:(pi + 1) * T]
                            S.op("act", lambda e, pap=pap, bank=bank, c0=c0, bias_ap=bias_ap: e.activation(
                                out=pap, in_=ps[bank][:, c0:T], func=AF.Exp, bias=bias_ap, scale=QK_SCALE),
                                 reads=[("ps", bank), "cstf"], writes=[("pt", pi)])
                            ptaps.append((pap, ("pt", pi)))
                        if pend is not None:
                            emit_av(pend, False)
                        pend = (j, c0, ptaps)
                        if deferred and (j == min(2, nkt - 1)):
                            deferred.pop(0)()
                    emit_av(pend, True)

                    _, r1, r1_r = next_tmp()
                    _, r2, r2_r = next_tmp()
                    _, t1, t1_r = next_tmp()
                    _, t2, t2_r = next_tmp()
                    S.op("dve", lambda e, r1=r1: e.reciprocal(out=r1, in_=ps[6][:, :]), reads=[("ps", 6)], writes=[r1_r])
                    S.op("act", lambda e, t1=t1: e.copy(out=t1, in_=ps[4][:, :]), reads=[("ps", 4)], writes=[t1_r])
                    S.op("dve", lambda e, r2=r2: e.reciprocal(out=r2, in_=ps[7][:, :]), reads=[("ps", 7)], writes=[r2_r])
                    S.op("act", lambda e, t2=t2: e.copy(out=t2, in_=ps[5][:, :]), reads=[("ps", 5)], writes=[t2_r])
                    S.op("dve", lambda e, t1=t1, r1=r1: e.tensor_tensor(out=t1, in0=t1, in1=r1, op=ALU.mult),
                         reads=[t1_r, r1_r], writes=[t1_r])
                    S.op("dve", lambda e, t2=t2, r2=r2: e.tensor_tensor(out=t2, in0=t2, in1=r2, op=ALU.mult),
                         reads=[t2_r, r2_r], writes=[t2_r])
                    S.op("dve", lambda e, t1=t1, t2=t2: e.scalar_tensor_tensor(
                        out=t1, in0=t2, scalar=DP("neg_lam"), in1=t1, op0=ALU.mult, op1=ALU.add),
                         reads=[t1_r, t2_r, "dp_lam"], writes=[t1_r])
                    _, osq, osq_r = next_tmpb()
                    S.op("act", lambda e, t1=t1, osq=osq: e.activation(out=osq, in_=t1, func=AF.Square),
                         reads=[t1_r], writes=[osq_r])

                    def post_b(h=h, t1=t1, t1_r=t1_r, osq=osq, osq_r=osq_r):
                        bank = next_bank(4)
                        mm(ps[bank][:, :], ones128[:], osq, True, True, reads=["ones128", osq_r], writes=[("ps", bank)])
                        _, rs, rs_r = next_tmp()
                        rsqrt_act(rs, rs_r, ps[bank][:, :], [("ps", bank)], rs, rs_r)
                        S.op("dve", lambda e: e.scalar_tensor_tensor(
                            out=attnT[:, h * T:(h + 1) * T], in0=t1, scalar=DP("gain08"), in1=rs,
                            op0=ALU.mult, op1=ALU.mult),
                             reads=[t1_r, rs_r, "dp_gain"], writes=[("attnT", h)])
                    deferred.append(post_b)

                wba, wra = use_unit("glu_a")
                wbb, wrb = use_unit("glu_b", hold=1)
                for ch in range(4):
                    ub = ch * UW
                    if I == 0:
                        S.op("dve", lambda e, ub=ub: e.memset(u[:, ub:ub + HALO], 0.0), writes=[("u", ch)])
                    else:
                        S.op("dve", lambda e, ub=ub: e.tensor_copy(out=u[:, ub:ub + HALO], in_=u[:, ub + T:ub + T + HALO]),
                             reads=[("u", ch)], writes=[("u", ch)])
                    bank_a = next_bank()
                    proj_group(bank_a, wba, wra, 512, ch * 128, xb, "xb", 8)
                    bank_b = next_bank()
                    proj_group(bank_b, wbb, wrb, 512, ch * 128, xb, "xb", 8)
                    _, sg, sg_r = next_tmp()
                    S.op("act", lambda e, sg=sg, bank_b=bank_b: e.activation(out=sg, in_=ps[bank_b][:, :], func=AF.Tanh,
                                                                             scale=0.5),
                         reads=[("ps", bank_b)], writes=[sg_r])
                    S.op("dve", lambda e, sg=sg, bank_a=bank_a, ub=ub: e.scalar_tensor_tensor(
                        out=u[:, ub + HALO:ub + HALO + T], in0=sg, scalar=1.0, in1=ps[bank_a][:, :],
                        op0=ALU.add, op1=ALU.mult),
                         reads=[sg_r, ("ps", bank_a)], writes=[("u", ch)])
                while deferred:
                    deferred.pop(0)()
                dump(tidx, 0, attnT[:], 2048, [("attnT", h) for h in range(4)])
                cbanks = [next_bank() for _ in range(4)]
                bm = next_bank()
                bq = next_bank()
                for ch in range(4):
                    ub = ch * UW
                    cbk = cbanks[ch]
                    for k in range(CONVK):
                        di = ch * CONVK + k
                        mm(ps[cbk][:, :], diag[:, di * 128:(di + 1) * 128], u[:, ub + k:ub + k + T], k == 0, k == CONVK - 1,
                           reads=["diag", ("u", ch)], writes=[("ps", cbk)])
                    _, cb, cb_r = next_tmpb()
                    _, cs, cs_r = next_tmpb()
                    S.op("act", lambda e, cb=cb, cbk=cbk, ch=ch: e.activation(out=cb, in_=ps[cbk][:, :], func=AF.Identity,
                                                                              bias=P("conv_b", ch)),
                         reads=[("ps", cbk), "par"], writes=[cb_r])
                    S.op("act", lambda e, cs=cs, cbk=cbk, ch=ch: e.activation(out=cs, in_=ps[cbk][:, :], func=AF.Square,
                                                                              bias=P("conv_b", ch)),
                         reads=[("ps", cbk), "par"], writes=[cs_r])
                    mm(ps[bm][:, :], ones512[:], cb, ch == 0, ch == 3, reads=["ones512", cb_r], writes=[("ps", bm)])
                    mm(ps[bq][:, :], ones512[:], cs, ch == 0, ch == 3, reads=["ones512", cs_r], writes=[("ps", bq)])
                S.op("act", lambda e, bm=bm: e.copy(out=mean_sb[:], in_=ps[bm][:, :]), reads=[("ps", bm)],
                     writes=["mean_sb"])
                _, tq, tq_r = next_tmp()
                S.op("dve", lambda e, tq=tq: e.tensor_tensor(out=tq, in0=mean_sb[:], in1=mean_sb[:], op=ALU.mult),
                     reads=["mean_sb"], writes=[tq_r])
                S.op("dve", lambda e, tq=tq, bq=bq: e.tensor_tensor(out=tq, in0=ps[bq][:, :], in1=tq, op=ALU.subtract),
                     reads=[("ps", bq), tq_r], writes=[tq_r])
                rsqrt_act(rstd_sb[:], "rstd_sb", tq, [tq_r], tq, tq_r)
                for ch in range(4):
                    cbk = cbanks[ch]
                    _, tc, tc_r = next_tmp()
                    S.op("dve", lambda e, tc=tc, cbk=cbk, ch=ch: e.scalar_tensor_tensor(
                        out=tc, in0=ps[cbk][:, :], scalar=P("conv_b", ch), in1=mean_sb[:], op0=ALU.add, op1=ALU.subtract),
                         reads=[("ps", cbk), "mean_sb", "par"], writes=[tc_r])
                    S.op("dve", lambda e, tc=tc: e.tensor_tensor(out=tc, in0=tc, in1=rstd_sb[:], op=ALU.mult),
                         reads=[tc_r, "rstd_sb"], writes=[tc_r])
                    S.op("act", lambda e, tc=tc, ch=ch: e.activation(
                        out=u2T[:, ch * T:(ch + 1) * T], in_=tc, func=AF.Silu, bias=P("conv_beta", ch),
                        scale=P("conv_g", ch)),
                         reads=[tc_r, "par"], writes=[("u2T", ch)])

                dump(tidx, 2048, u2T[:], 2048, [("u2T", c) for c in range(4)])
                for dc in range(8):
                    wb, wr = use_unit("mix%d" % dc)
                    b_ga = next_bank()
                    for kc in range(8):
                        mm(ps[b_ga][:, :], wring[:, wb + kc * 128:wb + (kc + 1) * 128], xb[:, kc * T:(kc + 1) * T],
                           kc == 0, kc == 7, reads=[wr, ("xb", kc)], writes=[("ps", b_ga)])
                    b_a = next_bank()
                    for kc in range(4):
                        mm(ps[b_a][:, :], wring[:, wb + 2048 + kc * 128:wb + 2048 + (kc + 1) * 128],
                           attnT[:, kc * T:(kc + 1) * T], kc == 0, kc == 3,
                           reads=[wr, ("attnT", kc)], writes=[("ps", b_a)])
                    b_gc = next_bank()
                    for kc in range(8):
                        mm(ps[b_gc][:, :], wring[:, wb + 1024 + kc * 128:wb + 1024 + (kc + 1) * 128],
                           xb[:, kc * T:(kc + 1) * T], kc == 0, kc == 7, reads=[wr, ("xb", kc)], writes=[("ps", b_gc)])
                    b_c = next_bank()
                    for kc in range(4):
                        mm(ps[b_c][:, :], wring[:, wb + 2560 + kc * 128:wb + 2560 + (kc + 1) * 128],
                           u2T[:, kc * T:(kc + 1) * T], kc == 0, kc == 3,
                           reads=[wr, ("u2T", kc)], writes=[("ps", b_c)])
                    _, tha, tha_r = next_tmp()
                    _, thc, thc_r = next_tmp()
                    _, ah, ah_r = next_tmp()
                    _, ch_, ch_r = next_tmp()
                    S.op("act", lambda e, tha=tha, b_ga=b_ga, dc=dc: e.activation(
                        out=tha, in_=ps[b_ga][:, :], func=AF.Tanh, bias=DP("b_gate_h", dc), scale=0.5),
                         reads=[("ps", b_ga), "dp_bg"], writes=[tha_r])
                    S.op("act", lambda e, ah=ah, b_a=b_a: e.activation(out=ah, in_=ps[b_a][:, :], func=AF.Identity, scale=0.5),
                         reads=[("ps", b_a)], writes=[ah_r])
                    S.op("act", lambda e, thc=thc, b_gc=b_gc, dc=dc: e.activation(
                        out=thc, in_=ps[b_gc][:, :], func=AF.Tanh, bias=DP("b_gate_h", 8 + dc), scale=0.5),
                         reads=[("ps", b_gc), "dp_bg"], writes=[thc_r])
                    S.op("act", lambda e, ch_=ch_, b_c=b_c, dc=dc: e.activation(
                        out=ch_, in_=ps[b_c][:, :], func=AF.Identity, bias=DP("bc_h", dc), scale=0.5),
                         reads=[("ps", b_c), "dp_bc"], writes=[ch_r])
                    S.op("dve", lambda e, tha=tha, ah=ah: e.scalar_tensor_tensor(
                        out=ah, in0=tha, scalar=1.0, in1=ah, op0=ALU.add, op1=ALU.mult),
                         reads=[tha_r, ah_r], writes=[ah_r])
                    S.op("dve", lambda e, thc=thc, ch_=ch_: e.scalar_tensor_tensor(
                        out=ch_, in0=thc, scalar=1.0, in1=ch_, op0=ALU.add, op1=ALU.mult),
                         reads=[thc_r, ch_r], writes=[ch_r])
                    S.op("dve", lambda e, ah=ah, ch_=ch_, dc=dc: e.tensor_tensor(
                        out=mixT[:, dc * T:(dc + 1) * T], in0=ah, in1=ch_, op=ALU.add),
                         reads=[ah_r, ch_r], writes=[("mixT", dc)])

                dump(tidx, 4096, mixT[:], 4096, [("mixT", c) for c in range(8)])
                for half in range(2):
                    wb, wr = use_unit("wo%d" % half)
                    for c4 in range(4):
                        dc = half * 4 + c4
                        bank = next_bank(6)
                        proj_group(bank, wb, wr, 512, c4 * 128, mixT, "mixT", 8)
                        slot = dc % 2
                        residual_and_stats(dc, bank, xst[:, slot * T:(slot + 1) * T], ("xst", slot))
                        if dc + 2 < 8:
                            xst_load(dc + 2)
                ln_stats_and_apply("ln1_g", "ln1_b", True, False, ctok0)
                dump(tidx, 8192, zx[:], 4096, [("zx", c) for c in range(8)])

                for g in range(6):
                    nch = 4 if g < 5 else 2
                    wbg, wrg = use_unit("fg%d" % g)
                    wbv, wrv = use_unit("fv%d" % g, hold=1)
                    ncu = nch * 128
                    for c in range(nch):
                        f = g * 4 + c
                        slot = f % 2
                        gb = slot * GW
                        bank_g = next_bank()
                        proj_group(bank_g, wbg, wrg, ncu, c * 128, x1b, "xb", 8)
                        bank_v = next_bank()
                        proj_group(bank_v, wbv, wrv, ncu, c * 128, x1b, "xb", 8)
                        if I == 0:
                            S.op("dve", lambda e, gb=gb: e.memset(graw[:, gb:gb + 2], 0.0),
                                 writes=[("graw", slot)])
                        else:
                            S.op("act", lambda e, gb=gb, f=f: e.copy(out=graw[:, gb:gb + 2], in_=ghalo[:, f * 2:f * 2 + 2]),
                                 reads=[("ghalo", f)], writes=[("graw", slot)])
                        S.op("act", lambda e, gb=gb, bank_g=bank_g: e.copy(out=graw[:, gb + 2:gb + 2 + T],
                                                                         in_=ps[bank_g][:, :]),
                             reads=[("ps", bank_g)], writes=[("graw", slot)])
                        S.op("act", lambda e, gb=gb, f=f: e.copy(out=ghalo[:, f * 2:f * 2 + 2], in_=graw[:, gb + T:gb + T + 2]),
                             reads=[("graw", slot)], writes=[("ghalo", f)])
                        _, gc_, gc_r = next_tmp()
                        S.op("act", lambda e, gc_=gc_, bank_g=bank_g, f=f: e.activation(
                            out=gc_, in_=ps[bank_g][:, :], func=AF.Identity, bias=P("ffn_b", f), scale=P("ffn_w", f * 3 + 2)),
                             reads=[("ps", bank_g), "par"], writes=[gc_r])
                        S.op("dve", lambda e, gc_=gc_, gb=gb, f=f: e.scalar_tensor_tensor(
                            out=gc_, in0=graw[:, gb + 1:gb + 1 + T], scalar=P("ffn_w", f * 3 + 1), in1=gc_,
                            op0=ALU.mult, op1=ALU.add),
                             reads=[("graw", slot), gc_r, "par"], writes=[gc_r])
                        S.op("dve", lambda e, gc_=gc_, gb=gb, f=f: e.scalar_tensor_tensor(
                            out=gc_, in0=graw[:, gb:gb + T], scalar=P("ffn_w", f * 3 + 0), in1=gc_,
                            op0=ALU.mult, op1=ALU.add),
                             reads=[("graw", slot), gc_r, "par"], writes=[gc_r])
                        S.op("act", lambda e, gc_=gc_: e.activation(out=gc_, in_=gc_, func=AF.Gelu),
                             reads=[gc_r], writes=[gc_r])
                        S.op("dve", lambda e, gc_=gc_, bank_v=bank_v, f=f: e.tensor_tensor(
                            out=hidT[:, f * T:(f + 1) * T], in0=gc_, in1=ps[bank_v][:, :], op=ALU.mult),
                             reads=[gc_r, ("ps", bank_v)], writes=[("hidT", f)])

                if tidx + 1 < NSEQ * NT:
                    xload(tidx + 1)
                for dc in range(8):
                    wb, wr = use_unit("dn%d" % dc)
                    bank = next_bank(6)
                    proj_group(bank, wb, wr, 128, 0, hidT, "hidT", NF)
                    residual_and_stats(dc, bank)
                ln_stats_and_apply("ln2_g", "ln2_b", False, True, ctok0)

        S.op("sp", None, reads=[("outd", dc) for dc in range(8)])
        if DEBUG:
            S.op("pool", None, reads=[("dbgd", i + 1) for i in range(dbgc["n"])])
        S.emit(nc, st)
    return nc


_NC_CACHE = {}
DEBUG = False
DBG_COLS = 2048 + 2048 + 4096 + 4096
_DBG = {}


def kernel(**inputs):
    x = np.asarray(inputs["x"], np.float32)
    wst = _pack_weights(*[np.asarray(inputs[k], np.float32) for k in
                          ("w_in", "w_attn_proj", "w_conv_proj", "w_out", "w_ffn_in", "w_ffn_down")])
    par = _pack_params({k: np.asarray(v, np.float32) for k, v in inputs.items()})
    cst = _make_consts()
    if "nc" not in _NC_CACHE:
        _NC_CACHE["nc"] = build_nc()
    nc = _NC_CACHE["nc"]
    in_maps = []
    for c in range(NCORES):
        xs = x[NSEQ * c:NSEQ * (c + 1)]
        xt = xs.reshape(NSEQ, NT, T, 8, 128).transpose(0, 1, 4, 3, 2)
        xt = np.ascontiguousarray(xt).reshape(NSEQ * NT, 128, 8 * T)
        in_maps.append({"xT": xt, "wst": wst, "par": par, "cst": cst})
    res = run_bass_kernel_spmd(nc, in_maps, core_ids=list(range(NCORES)))
    if DEBUG:
        _DBG["dbg"] = np.asarray(res.results[0]["dbg"])
    out = np.empty((16, SEQ, D), np.float32)
    for c in range(NCORES):
        oT = np.asarray(res.results[c]["outT"])
        out[NSEQ * c:NSEQ * (c + 1)] = oT.T.reshape(NSEQ, SEQ, D)
    return out
```
